# Optimizing a Trainium2 kernel written in Bass

```python
import numpy as np
import jax
import jax.numpy as jnp
from jax import lax

D_MODEL = 4096
BATCH = 2
SEQ = 8192
DEPTH = 4

HEAD_DIM = 128
ROPE_THETA = 10000.0
NORM_EPS = 1e-6
Q_BLOCK = 128
N_BRANCH = 4
BRANCH_WIDTH = D_MODEL // 4
DSA_HEADS = BRANCH_WIDTH // HEAD_DIM
IDX_HEADS = 16
IDX_DIM = 64
DSA_TOPK_MAX = 256
CONV_WIDTH = BRANCH_WIDTH
CONV_K = 3
FOX_HEADS = BRANCH_WIDTH // HEAD_DIM
HGRN_HEADS = BRANCH_WIDTH // HEAD_DIM
HGRN_DK = 128
HGRN_DV = 128
HGRN_CHUNK = 64
MERGE_BLOCKS = 16
MERGE_BLOCK_DIM = D_MODEL // MERGE_BLOCKS

IN_LAYOUT = (
    ("a_q", DSA_HEADS * HEAD_DIM), ("a_k", HEAD_DIM), ("a_v", HEAD_DIM),
    ("a_iq", IDX_HEADS * IDX_DIM), ("a_ik", IDX_DIM), ("a_iw", IDX_HEADS), ("a_g", BRANCH_WIDTH),
    ("b_b", CONV_WIDTH), ("b_c", CONV_WIDTH), ("b_x", CONV_WIDTH), ("b_g", BRANCH_WIDTH),
    ("c_q", FOX_HEADS * HEAD_DIM), ("c_k", FOX_HEADS * HEAD_DIM), ("c_v", FOX_HEADS * HEAD_DIM),
    ("c_f", FOX_HEADS), ("c_g", BRANCH_WIDTH),
    ("d_q", HGRN_HEADS * HGRN_DK), ("d_f", HGRN_HEADS * HGRN_DK), ("d_i", HGRN_HEADS * HGRN_DV),
    ("d_g", BRANCH_WIDTH),
)
IN_WIDTH = int(sum(w for _, w in IN_LAYOUT))
IN_OFFSETS = tuple(int(o) for o in np.cumsum([w for _, w in IN_LAYOUT])[:-1])

kernel_name = "hybrid_dsa_conv_fox_hgrn2_gated_merge"


def rms_norm(x, w):
    xf = x.astype(jnp.float32)
    y = xf * lax.rsqrt(jnp.mean(xf * xf, axis=-1, keepdims=True) + NORM_EPS)
    return (y * w.astype(jnp.float32)).astype(x.dtype)


def rotary(x, pos):
    half = x.shape[-1] // 2
    inv = ROPE_THETA ** (-jnp.arange(half, dtype=jnp.float32) / half)
    ang = pos.astype(jnp.float32)[:, None] * inv[None, :]
    cos = jnp.cos(ang)[None, :, None, :]
    sin = jnp.sin(ang)[None, :, None, :]
    xf = x.astype(jnp.float32)
    x1, x2 = xf[..., :half], xf[..., half:]
    return jnp.concatenate([x1 * cos - x2 * sin, x1 * sin + x2 * cos], axis=-1).astype(x.dtype)


def to_blocks(t):
    b, s = t.shape[:2]
    return jnp.moveaxis(t.reshape((b, s // Q_BLOCK, Q_BLOCK) + t.shape[2:]), 1, 0)


def from_blocks(t):
    t = jnp.moveaxis(t, 0, 1)
    return t.reshape((t.shape[0], t.shape[1] * t.shape[2]) + t.shape[3:])


def dsa_sparse_attention(q, k, v, iq, ik, iw):
    b, s = q.shape[:2]
    topk = min(DSA_TOPK_MAX, s // 4)
    pos = jnp.arange(s)
    scale = HEAD_DIM ** -0.5
    iw = iw.astype(jnp.float32) * (IDX_HEADS ** -0.5 * IDX_DIM ** -0.5)
    gather = jax.vmap(lambda t, i: t[i])

    def block(args):
        qb, iqb, iwb, tq = args
        idx_logit = jnp.einsum('bqhd,bkd->bqhk', iqb, ik).astype(jnp.float32)
        score = jnp.einsum('bqh,bqhk->bqk', iwb, jax.nn.relu(idx_logit))
        causal = pos[None, :] <= tq[:, None]
        score = jnp.where(causal[None], score, -jnp.inf)
        _, sel = lax.top_k(score, topk)
        valid = sel <= tq[None, :, None]
        kg = gather(k, sel)
        vg = gather(v, sel)
        logit = jnp.einsum('bqhd,bqnd->bqhn', qb, kg).astype(jnp.float32) * scale
        logit = jnp.where(valid[:, :, None, :], logit, -jnp.inf)
        p = jax.nn.softmax(logit, axis=-1).astype(v.dtype)
        return jnp.einsum('bqhn,bqnd->bqhd', p, vg)

    out = lax.map(block, (to_blocks(q), to_blocks(iq), to_blocks(iw), pos.reshape(-1, Q_BLOCK)))
    return from_blocks(out).reshape(b, s, -1)


def short_gated_conv(b_gate, c_gate, x_in, conv_w):
    u = c_gate * x_in
    y = lax.conv_general_dilated(
        u, conv_w[:, None, :].astype(u.dtype), window_strides=(1,),
        padding=[(CONV_K - 1, 0)], dimension_numbers=('NWC', 'WIO', 'NWC'),
        feature_group_count=u.shape[-1])
    return b_gate * y


def forgetting_attention(q, k, v, log_f):
    s = q.shape[1]
    pos = jnp.arange(s)
    scale = HEAD_DIM ** -0.5
    c = jnp.cumsum(log_f, axis=1)
    c_key = jnp.transpose(c, (0, 2, 1))

    def block(args):
        qb, cb, tq = args
        logit = jnp.einsum('bqhd,bkhd->bhqk', qb, k).astype(jnp.float32) * scale
        logit = logit + jnp.transpose(cb, (0, 2, 1))[..., None] - c_key[:, :, None, :]
        causal = pos[None, :] <= tq[:, None]
        logit = jnp.where(causal[None, None], logit, -jnp.inf)
        p = jax.nn.softmax(logit, axis=-1).astype(v.dtype)
        return jnp.einsum('bhqk,bkhd->bqhd', p, v)

    out = lax.map(block, (to_blocks(q), to_blocks(c), pos.reshape(-1, Q_BLOCK)))
    return from_blocks(out)


def hgrn2_recurrence(q, k, v, log_f):
    b, s, h, dk = q.shape
    dv = v.shape[-1]
    nc = s // HGRN_CHUNK

    def chunks(t):
        return t.reshape(b, nc, HGRN_CHUNK, h, t.shape[-1]).transpose(1, 0, 3, 2, 4)

    tri = jnp.tril(jnp.ones((HGRN_CHUNK, HGRN_CHUNK), dtype=bool))

    def step(state, inp):
        qc, kc, vc, gc = inp
        bcum = jnp.cumsum(gc, axis=2)
        o_inter = jnp.einsum('bhtd,bhde->bhte', qc * jnp.exp(bcum), state)
        diff = bcum[:, :, :, None, :] - bcum[:, :, None, :, :]
        decay = jnp.exp(jnp.where(tri[:, :, None], diff, -jnp.inf))
        att = jnp.einsum('bhtd,bhtsd,bhsd->bhts', qc, decay, kc)
        o_intra = jnp.einsum('bhts,bhse->bhte', att, vc)
        b_last = bcum[:, :, -1:, :]
        new_state = (jnp.exp(b_last[:, :, 0, :])[..., None] * state
                     + jnp.einsum('bhsd,bhse->bhde', kc * jnp.exp(b_last - bcum), vc))
        return new_state, o_inter + o_intra

    state0 = jnp.zeros((b, h, dk, dv), jnp.float32)
    _, o = lax.scan(step, state0, (chunks(q), chunks(k), chunks(v), chunks(log_f)))
    return o.transpose(1, 0, 3, 2, 4).reshape(b, s, h, dv)


def hybrid_layer(x, norm_w, w_in, fox_f_bias, conv_w, hgrn_lb, hgrn_norm_w, w_branch, w_merge, b_merge, w_out):
    b, s, d = x.shape
    pos = jnp.arange(s)
    h = rms_norm(x, norm_w)
    u = h @ w_in
    (a_q, a_k, a_v, a_iq, a_ik, a_iw, a_g,
     b_b, b_c, b_x, b_g,
     c_q, c_k, c_v, c_f, c_g,
     d_q, d_f, d_i, d_g) = jnp.split(u, IN_OFFSETS, axis=-1)

    def heads(t, n):
        return t.reshape(b, s, n, -1)

    qa = rotary(heads(a_q, DSA_HEADS), pos)
    ka = rotary(a_k[:, :, None, :], pos)[:, :, 0]
    iq = rotary(heads(a_iq, IDX_HEADS), pos)
    ik = rotary(a_ik[:, :, None, :], pos)[:, :, 0]
    y_a = dsa_sparse_attention(qa, ka, a_v, iq, ik, a_iw) * jax.nn.silu(a_g)

    y_b = short_gated_conv(b_b, b_c, b_x, conv_w) * jax.nn.silu(b_g)

    log_f_c = jax.nn.log_sigmoid(c_f.astype(jnp.float32) + fox_f_bias.astype(jnp.float32))
    o_c = forgetting_attention(heads(c_q, FOX_HEADS), heads(c_k, FOX_HEADS), heads(c_v, FOX_HEADS), log_f_c)
    y_c = o_c.reshape(b, s, -1) * jax.nn.silu(c_g)

    lb = hgrn_lb.astype(jnp.float32)
    f = lb + (1.0 - lb) * jax.nn.sigmoid(d_f.astype(jnp.float32))
    o_d = hgrn2_recurrence(heads(d_q.astype(jnp.float32), HGRN_HEADS), heads(1.0 - f, HGRN_HEADS),
                           heads(d_i.astype(jnp.float32), HGRN_HEADS), heads(jnp.log(f), HGRN_HEADS))
    o_d = rms_norm(o_d, hgrn_norm_w.reshape(HGRN_HEADS, HGRN_DV)).astype(x.dtype)
    y_d = o_d.reshape(b, s, -1) * jax.nn.silu(d_g)

    h_blk = h.reshape(b, s, MERGE_BLOCKS, MERGE_BLOCK_DIM)

    def gated_branch(i, y):
        g = jnp.einsum('bsnc,ncd->bsnd', h_blk, w_merge[i]).reshape(b, s, d)
        return jax.nn.sigmoid(g + b_merge[i]) * (y @ w_branch[i])

    merged = gated_branch(0, y_a) + gated_branch(1, y_b) + gated_branch(2, y_c) + gated_branch(3, y_d)
    return x + merged @ w_out


def setup_inputs(seed: int = 0) -> dict:
    key = jax.random.key(seed)
    ks = jax.random.split(key, 12)

    def nrm(k, shape, scale):
        return jax.random.normal(k, shape, jnp.float32) * scale

    return {
        "x": nrm(ks[0], (BATCH, SEQ, D_MODEL), 1.0),
        "norm_w": 1.0 + nrm(ks[1], (DEPTH, D_MODEL), 0.02),
        "w_in": nrm(ks[2], (DEPTH, D_MODEL, IN_WIDTH), D_MODEL ** -0.5),
        "fox_f_bias": 1.0 + nrm(ks[3], (DEPTH, FOX_HEADS), 0.1),
        "conv_w": nrm(ks[4], (DEPTH, CONV_K, CONV_WIDTH), CONV_K ** -0.5),
        "hgrn_gamma": nrm(ks[5], (DEPTH, HGRN_HEADS * HGRN_DK), 1.0),
        "hgrn_norm_w": 1.0 + nrm(ks[6], (DEPTH, HGRN_HEADS * HGRN_DV), 0.02),
        "w_branch": nrm(ks[7], (DEPTH, N_BRANCH, BRANCH_WIDTH, D_MODEL), BRANCH_WIDTH ** -0.5),
        "w_merge": nrm(ks[8], (DEPTH, N_BRANCH, MERGE_BLOCKS, MERGE_BLOCK_DIM, MERGE_BLOCK_DIM), MERGE_BLOCK_DIM ** -0.5),
        "b_merge": nrm(ks[9], (DEPTH, N_BRANCH, D_MODEL), 0.01),
        "w_out": nrm(ks[10], (DEPTH, D_MODEL, D_MODEL), D_MODEL ** -0.5),
        "final_norm_w": 1.0 + nrm(ks[11], (D_MODEL,), 0.02),
    }


def reference(x, norm_w, w_in, fox_f_bias, conv_w, hgrn_gamma, hgrn_norm_w, w_branch, w_merge, b_merge, w_out, final_norm_w):
    gam = jax.nn.softmax(hgrn_gamma.astype(jnp.float32), axis=0)
    lower_bounds = jnp.cumsum(gam, axis=0) - gam[0]
    for layer in range(DEPTH):
        x = hybrid_layer(x, norm_w[layer], w_in[layer], fox_f_bias[layer], conv_w[layer],
                         lower_bounds[layer], hgrn_norm_w[layer], w_branch[layer],
                         w_merge[layer], b_merge[layer], w_out[layer])
    return rms_norm(x, final_norm_w)
```

```python
import numpy as np
from contextlib import ExitStack
import concourse.bass as bass
import concourse.mybir as mybir
from concourse.bass_utils import run_bass_kernel_spmd

F32 = mybir.dt.float32
BF16 = mybir.dt.bfloat16
AF = mybir.ActivationFunctionType
ALU = mybir.AluOpType
AX = mybir.AxisListType
P = 128
NEG = -30000.0


class Cfg:
    def __init__(self, S=8192, D=4096, DEPTH=4):
        self.S, self.D, self.DEPTH = S, D, DEPTH
        self.BW = D // 4
        self.H = self.BW // 128
        self.IH, self.ID = 16, 64
        self.MB = 16
        self.bd = D // 16
        self.KC = D // 128
        self.NTT = S // 128
        self.NQC = S // 512
        self.TOPK = min(256, S // 4)
        BW, H = self.BW, self.H
        lay = [("a_q", BW), ("a_k", 128), ("a_v", 128), ("a_iq", 1024), ("a_ik", 64), ("a_iw", 16),
               ("a_g", BW), ("b_b", BW), ("b_c", BW), ("b_x", BW), ("b_g", BW),
               ("c_q", BW), ("c_k", BW), ("c_v", BW), ("c_f", H), ("c_g", BW),
               ("d_q", BW), ("d_f", BW), ("d_i", BW), ("d_g", BW)]
        self.off = {}
        o = 0
        for n, w in lay:
            self.off[n] = (o, w)
            o += w
        self.INW = o


class Prog:
    ENGS = ("pe", "act", "dve", "pool", "sp")

    def __init__(self, nc, stack):
        self.nc = nc
        self.streams = {e: [] for e in self.ENGS}
        self.sem = {e: stack.enter_context(nc.semaphore("sem_" + e)) for e in ("pe", "act", "dve", "pool")}
        self.cnt = {e: 0 for e in ("pe", "act", "dve", "pool")}
        self.dq = {}
        for q, n in (("sp", 12), ("pool", 6)):
            self.dq[q] = dict(sems=[stack.enter_context(nc.semaphore("dq_%s_%d" % (q, i))) for i in range(n)],
                              uses=[0] * n, idx=0)
        self.semobj = {}
        for e, s in self.sem.items():
            self.semobj[e] = s
        for q, d in self.dq.items():
            for i, s in enumerate(d["sems"]):
                self.semobj["%s#%d" % (q, i)] = s
        self.waited = {}
        self.lastw = {}
        self.reads = {}
        self.ninst = 0

    def _need(self, eng, toks, attach=False):
        att = None
        for k, v in toks.items():
            if v <= 0:
                continue
            if eng == "pe" and k == "pe":
                continue
            if self.waited.get((eng, k), 0) < v:
                self.waited[(eng, k)] = v
                so = self.semobj[k]
                if attach and att is None and eng != "pe":
                    att = (so, v)
                    continue
                self.streams[eng].append(lambda e, so=so, v=v: e.wait_ge(so, v))
                self.ninst += 1
        return att

    def _deps(self, reads, writes):
        toks = {}

        def add(d):
            for k, v in d.items():
                if toks.get(k, 0) < v:
                    toks[k] = v
        for r in reads:
            add(self.lastw.get(r, {}))
        for w in writes:
            add(self.lastw.get(w, {}))
            add(self.reads.get(w, {}))
        return toks

    def _mark(self, key, val, reads, writes):
        for r in reads:
            d = self.reads.setdefault(r, {})
            if d.get(key, 0) < val:
                d[key] = val
        for w in writes:
            d = self.lastw.setdefault(w, {})
            if d.get(key, 0) < val:
                d[key] = val

    def op(self, eng, fn, reads=(), writes=()):
        att = self._need(eng, self._deps(reads, writes), attach=True)
        self.cnt[eng] += 1
        so = self.sem[eng]
        if att is None:
            self.streams[eng].append(lambda e, fn=fn, so=so: fn(e).then_inc(so, 1))
        else:
            def emit_(e, fn=fn, so=so, att=att):
                i = fn(e)
                i.wait_op(att[0], att[1], "sem-ge")
                i.then_inc(so, 1)
            self.streams[eng].append(emit_)
        self._mark(eng, self.cnt[eng], reads, writes)
        self.ninst += 1

    def dma(self, q, out, in_, reads=(), writes=()):
        d = self.dq[q]
        i = d["idx"]
        d["idx"] = (i + 1) % len(d["sems"])
        key = "%s#%d" % (q, i)
        toks = self._deps(reads, writes)
        if toks.get(key, 0) < 16 * d["uses"][i]:
            toks[key] = 16 * d["uses"][i]
        att = self._need(q, toks, attach=True)
        d["uses"][i] += 1
        so = d["sems"][i]

        def emit_(e, out=out, in_=in_, so=so, att=att):
            ins = e.dma_start(out=out, in_=in_)
            if att is not None:
                ins.wait_op(att[0], att[1], "sem-ge")
            ins.then_inc(so, 16)
        self.streams[q].append(emit_)
        self._mark(key, 16 * d["uses"][i], reads, writes)
        self.ninst += 1

    def finish_wait(self, eng, res):
        self._need(eng, self._deps(res, ()))

    def emit(self):
        nc = self.nc
        with nc.Block() as block:
            @block.sync
            def _(e):
                for f in self.streams["sp"]:
                    f(e)

            @block.tensor
            def _(e):
                for f in self.streams["pe"]:
                    f(e)

            @block.scalar
            def _(e):
                for f in self.streams["act"]:
                    f(e)

            @block.vector
            def _(e):
                for f in self.streams["dve"]:
                    f(e)

            @block.gpsimd
            def _(e):
                for f in self.streams["pool"]:
                    f(e)


def build_nc(cfg):
    S, D, L = cfg.S, cfg.D, cfg.DEPTH
    BW, H, KC, NTT, NQC, bd = cfg.BW, cfg.H, cfg.KC, cfg.NTT, cfg.NQC, cfg.bd
    INW = cfg.INW
    nc = bass.Bass("TRN2", target_bir_lowering=False)

    def din(name, shape, dt=F32):
        return nc.dram_tensor(name, list(shape), dt, kind="ExternalInput")

    def dscr(name, shape, dt):
        return nc.dram_tensor(name, list(shape), dt, kind="Internal")

    x_in = din("x", [S, D])
    norm_w = din("norm_w", [L, D])
    w_in = din("w_in", [L, D, INW])
    fox_b = din("fox_f_bias", [L, H])
    conv_w = din("conv_w", [L, 3, BW])
    LT = getattr(cfg, "LT", L)
    gamma = din("hgrn_gamma", [LT, BW])
    selm = din("selm", [P, LT]) if getattr(cfg, "unfused", False) else None
    hnorm_w = din("hgrn_norm_w", [L, BW])
    w_branch = din("w_branch", [L, 4, BW, D])
    w_merge = din("w_merge", [L, 4, 16, bd, bd])
    b_merge = din("b_merge", [L, 4, D])
    w_out = din("w_out", [L, D, D])
    fin_w = din("final_norm_w", [1, D])
    rope_t = din("rope_t", [4, P, S])
    ident_b = din("ident_b", [P, P], BF16)
    ident_f = din("ident_f", [P, P])
    cmask_fm = din("cmask_fm", [4, P, 512], BF16)
    cmask_tm = din("cmask_tm", [P, P])
    tril01 = din("tril01", [64, 64])
    y_out = nc.dram_tensor("y", [S, D], F32, kind="ExternalOutput")

    xres = dscr("xres", [S, D], F32)
    hT = dscr("hT", [D, S], BF16)
    sc = {}
    for n in ("a_q", "a_k", "a_iq", "a_g", "b_g", "c_q", "c_k", "c_g", "d_g"):
        sc[n] = dscr("s_" + n, [cfg.off[n][1], S], BF16)
    sc["a_ik"] = dscr("s_a_ik", [128, S], BF16)
    for n in ("b_b", "b_c", "b_x", "d_q", "d_f"):
        sc[n] = dscr("s_" + n, [BW, S], F32)
    sc["c_f"] = dscr("s_c_f", [H, S], F32)
    for n in ("a_v", "c_v", "d_i"):
        sc[n] = dscr("s_" + n, [S, cfg.off[n][1]], BF16)
    sc["a_iw"] = dscr("s_a_iw", [S, 16], F32)
    ccum = dscr("ccum", [H, S], F32)
    yT = dscr("yT", [4 * BW, S], BF16)
    mT = dscr("mT", [D, S], BF16)

    stack = ExitStack()
    with stack:
        pg = Prog(nc, stack)

        def sb(name, shape, dt):
            return stack.enter_context(nc.sbuf_tensor(name, list(shape), dt))

        def ps(name, shape, dt):
            return stack.enter_context(nc.psum_tensor(name, list(shape), dt))

        Wt = sb("Wt", [P, KC, 512], BF16)
        HT = [sb("HT%d" % i, [P, KC, 512], BF16) for i in range(2)]
        SC = sb("SC", [P, max(S, D)], F32)
        xt = sb("xt", [P, D], F32)
        hb = sb("hb", [P, D], BF16)
        wf = [sb("wf%d" % i, [P, 520], F32) for i in range(8)]
        wb = [sb("wb%d" % i, [P, 1024], BF16) for i in range(8)]
        small = sb("small", [P, 64], F32)
        idb = sb("idb", [P, P], BF16)
        idf = sb("idf", [P, P], F32)
        onesb = sb("onesb", [P, P], BF16)
        cm_fm = sb("cm_fm", [P, 4, 512], BF16)
        cm_tm = sb("cm_tm", [P, P], F32)
        tril = sb("tril", [64, 64], F32)
        rope = sb("rope", [P, 2, 512], F32)
        colp = sb("colp", [P, 16 * KC], F32)
        negc = sb("negc", [P, NTT, H], F32)
        iwt = sb("iwt", [P, 16], F32)
        Sst = sb("Sst", [P, P], F32)
        Sbf = sb("Sbf", [P, P], BF16)
        pA = [ps("pA%d" % i, [P, 512], F32) for i in range(4)]
        pB = [ps("pB%d" % i, [P, 512], F32) for i in range(2)]
        pT = [ps("pT%d" % i, [P, 1024], BF16) for i in range(2)]

        pg.dma("sp", idb[:, :], ident_b[:, :], writes=["idb"])
        pg.dma("sp", idf[:, :], ident_f[:, :], writes=["idf"])
        pg.dma("sp", cm_fm[:, :, :], cmask_fm.ap().rearrange("m p q -> p m q"), writes=["cm_fm"])
        pg.dma("sp", cm_tm[:, :], cmask_tm[:, :], writes=["cm_tm"])
        pg.dma("sp", tril[:, :], tril01[:, :], writes=["tril"])
        pg.op("dve", lambda e: e.memset(onesb[:, :], 1.0), writes=["onesb"])

        for tt in range(NTT):
            pg.dma("sp", xres[tt * P:(tt + 1) * P, :], x_in[tt * P:(tt + 1) * P, :], writes=["xres"])

        hT_v = hT.ap().rearrange("(kc p) t -> p kc t", p=P)
        mT_v = mT.ap().rearrange("(kc p) t -> p kc t", p=P)
        HTf = [h[:, :, :].rearrange("p a b -> p (a b)") for h in HT]
        Wtf = Wt[:, :, :].rearrange("p a b -> p (a b)")
        rowst = sb("rowst", [P, P], F32)
        lbcol = sb("lbcol", [P, 2 * L * H], F32)
        att_bf = sb("att_bf", [64, 64], BF16)
        cqt = sb("cqt", [P, 512], F32)
        maskT = Wtf[:, 0:NTT * P].rearrange("p (a b) -> p a b", b=P)
        dsa_iq = Wtf[:, NTT * P:NTT * P + 1024]
        dsa_qa = Wtf[:, NTT * P + 1024:NTT * P + 2048]
        dsa_ga = Wtf[:, NTT * P + 2048:NTT * P + 3072]
        onesf = sb("onesf", [P, 512], F32)
        pg.op("dve", lambda e: e.memset(onesf[:, :], 1.0), writes=["onesf"])
        rr = {"wf": 0, "wb": 0}

        def load_cols(dst, src_rows, n):
            pg.dma("sp", rowst[0:n, :], src_rows, writes=["rowst"])
            pg.op("pe", lambda e: e.transpose(out=pB[0][:, 0:n], in_=rowst[0:n, :], identity=idf[0:n, 0:n]),
                  reads=["rowst", "idf"], writes=["pB0"])
            pg.op("dve", lambda e: e.tensor_copy(out=dst, in_=pB[0][:, 0:n]), reads=["pB0"], writes=[dst.tensor.name])

        def load_w_cast(dst3, src2, res):
            nk = dst3.shape[1]
            srcv = src2.rearrange("(kc p) n -> p kc n", p=P)
            for k0 in range(0, nk, 8):
                k1 = min(nk, k0 + 8)
                pg.dma("pool", dst3[:, k0:k1, :], srcv[:, k0:k1, :], writes=[res])

        if getattr(cfg, "unfused", False):
            lbtmp = sb("lbtmp", [P, LT * H], F32)
            selt = sb("selt", [P, LT], F32)
            pg.dma("sp", selt[:, :], selm[:, :], writes=["selt"])
            for j in range(LT):
                load_cols(lbtmp[:, j * H:(j + 1) * H], gamma[j, :].rearrange("(h p) -> h p", p=P), H)
            pg.op("act", lambda e: e.activation(out=lbtmp[:, :], in_=lbtmp[:, :], func=AF.Exp), reads=["lbtmp"], writes=["lbtmp"])
            pg.op("dve", lambda e: e.tensor_copy(out=small[:, 40:40 + H], in_=lbtmp[:, 0:H]), reads=["lbtmp"], writes=["small"])
            for j in range(1, LT):
                pg.op("dve", lambda e, j=j: e.tensor_tensor(out=small[:, 40:40 + H], in0=small[:, 40:40 + H],
                                                            in1=lbtmp[:, j * H:(j + 1) * H], op=ALU.add),
                      reads=["lbtmp", "small"], writes=["small"])
            pg.op("dve", lambda e: e.reciprocal(out=small[:, 40:40 + H], in_=small[:, 40:40 + H]), reads=["small"], writes=["small"])
            pg.op("dve", lambda e: e.memset(lbcol[:, 0:H], 0.0), writes=["lbcol"])
            for j in range(LT):
                pg.op("dve", lambda e, j=j: e.tensor_tensor(out=lbtmp[:, j * H:(j + 1) * H], in0=lbtmp[:, j * H:(j + 1) * H],
                                                            in1=small[:, 40:40 + H], op=ALU.mult),
                      reads=["lbtmp", "small"], writes=["lbtmp"])
                pg.op("dve", lambda e, j=j: e.scalar_tensor_tensor(out=lbcol[:, 0:H], in0=lbtmp[:, j * H:(j + 1) * H], scalar=selt[:, j:j + 1],
                                                                   in1=lbcol[:, 0:H], op0=ALU.mult, op1=ALU.add),
                      reads=["lbtmp", "selt", "lbcol"], writes=["lbcol"])
            pg.op("dve", lambda e: e.tensor_scalar(out=lbcol[:, H:2 * H], in0=lbcol[:, 0:H], scalar1=-1.0, scalar2=1.0,
                                                   op0=ALU.mult, op1=ALU.add), reads=["lbcol"], writes=["lbcol"])
        else:
            for l in range(L):
                load_cols(lbcol[:, l * H:(l + 1) * H], gamma[l, :].rearrange("(h p) -> h p", p=P), H)
            pg.op("act", lambda e: e.activation(out=lbcol[:, 0:L * H], in_=lbcol[:, 0:L * H], func=AF.Exp),
                  reads=["lbcol"], writes=["lbcol"])
            pg.op("dve", lambda e: e.tensor_copy(out=small[:, 40:40 + H], in_=lbcol[:, 0:H]), reads=["lbcol"], writes=["small"])
            for l in range(1, L):
                pg.op("dve", lambda e, l=l: e.tensor_tensor(out=small[:, 40:40 + H], in0=small[:, 40:40 + H],
                                                            in1=lbcol[:, l * H:(l + 1) * H], op=ALU.add),
                      reads=["lbcol", "small"], writes=["small"])
            pg.op("dve", lambda e: e.reciprocal(out=small[:, 40:40 + H], in_=small[:, 40:40 + H]), reads=["small"], writes=["small"])
            for l in range(L):
                pg.op("dve", lambda e, l=l: e.tensor_tensor(out=lbcol[:, l * H:(l + 1) * H], in0=lbcol[:, l * H:(l + 1) * H],
                                                            in1=small[:, 40:40 + H], op=ALU.mult),
                      reads=["lbcol", "small"], writes=["lbcol"])
            pg.op("dve", lambda e: e.memset(lbcol[:, 0:H], 0.0), reads=["lbcol"], writes=["lbcol"])
            for l in range(2, L):
                pg.op("dve", lambda e, l=l: e.tensor_tensor(out=lbcol[:, l * H:(l + 1) * H], in0=lbcol[:, l * H:(l + 1) * H],
                                                            in1=lbcol[:, (l - 1) * H:l * H], op=ALU.add),
                      reads=["lbcol"], writes=["lbcol"])
            pg.op("dve", lambda e: e.tensor_scalar(out=lbcol[:, L * H:2 * L * H], in0=lbcol[:, 0:L * H], scalar1=-1.0, scalar2=1.0,
                                                   op0=ALU.mult, op1=ALU.add), reads=["lbcol"], writes=["lbcol"])

        def stage_norm(src, wrow, final=False):
            load_cols(colp[:, 0:KC], wrow.rearrange("o (kc p) -> (o kc) p", p=P), KC)
            if final:
                pg.dma("sp", SC[:, 0:D], wrow.to_broadcast([P, D]), writes=["SC"])
            for tt in range(NTT):
                pg.dma("sp", xt[:, :], src[tt * P:(tt + 1) * P, :], reads=["xres"], writes=["xt"])
                pg.op("act", lambda e: e.activation(out=hb[:, :], in_=xt[:, :], func=AF.Square, accum_out=small[:, 0:1]),
                      reads=["xt"], writes=["hb", "small"])
                pg.op("dve", lambda e: e.tensor_scalar(out=small[:, 1:2], in0=small[:, 0:1], scalar1=1.0 / D, scalar2=1e-6,
                                                       op0=ALU.mult, op1=ALU.add), reads=["small"], writes=["small"])
                pg.op("act", lambda e: e.activation(out=small[:, 2:3], in_=small[:, 1:2], func=AF.Sqrt),
                      reads=["small"], writes=["small"])
                pg.op("dve", lambda e: e.reciprocal(out=small[:, 3:4], in_=small[:, 2:3]), reads=["small"], writes=["small"])
                if final:
                    pg.op("dve", lambda e: e.scalar_tensor_tensor(out=xt[:, :], in0=xt[:, :], scalar=small[:, 3:4], in1=SC[:, 0:D],
                                                                  op0=ALU.mult, op1=ALU.mult),
                          reads=["xt", "small", "SC"], writes=["xt"])
                    pg.dma("sp", y_out[tt * P:(tt + 1) * P, :], xt[:, :], reads=["xt"], writes=["y"])
                    continue
                pg.op("dve", lambda e: e.tensor_scalar(out=hb[:, :], in0=xt[:, :], scalar1=small[:, 3:4], scalar2=None, op0=ALU.mult),
                      reads=["xt", "small"], writes=["hb"])
                j = tt % 4
                for k0 in range(0, KC, 8):
                    nk = min(8, KC - k0)
                    pt = pT[(k0 // 8) % 2]
                    ptn = "pT%d" % ((k0 // 8) % 2)
                    for kk in range(nk):
                        pg.op("pe", lambda e, kk=kk, k0=k0, pt=pt: e.transpose(out=pt[:, kk * P:(kk + 1) * P],
                                                                               in_=hb[:, (k0 + kk) * P:(k0 + kk + 1) * P], identity=idb[:, :]),
                              reads=["hb", "idb"], writes=[ptn])
                    for kk in range(nk):
                        pg.op("dve", lambda e, kk=kk, k0=k0, pt=pt, j=j: e.tensor_scalar(
                            out=Wt[:, k0 + kk, j * P:(j + 1) * P], in0=pt[:, kk * P:(kk + 1) * P],
                            scalar1=colp[:, k0 + kk:k0 + kk + 1], scalar2=None, op0=ALU.mult),
                            reads=[ptn, "colp"], writes=["Wt"])
                if j == 3:
                    tq = tt // 4
                    pg.dma("sp", hT_v[:, :, tq * 512:(tq + 1) * 512], Wt[:, :, :], reads=["Wt"], writes=["hT"])

        def nxt(kind):
            n = 8
            i = rr[kind]
            rr[kind] = (i + 1) % n
            return (wf[i], "wf%d" % i) if kind == "wf" else (wb[i], "wb%d" % i)

        ROPE_SLOTS = {"a_q": 0, "a_k": 0, "a_iq": 2, "a_ik": 2}
        FM_POST = {"a_q": "rope", "a_k": "rope", "a_iq": "rope", "a_ik": "rope",
                   "a_g": "silu", "b_g": "silu", "c_g": "silu", "d_g": "silu",
                   "b_b": "f32", "b_c": "f32", "b_x": "f32", "d_q": "f32", "d_f": "f32", "c_f": "f32",
                   "c_q": "bf16", "c_k": "bf16"}

        def load_ht(tq, srcv=None, res="hT"):
            ht = HT[tq % 2]
            pg.dma("sp", ht[:, :, :], (srcv if srcv is not None else hT_v)[:, :, tq * 512:(tq + 1) * 512],
                   reads=[res], writes=["HT%d" % (tq % 2)])
            return ht, "HT%d" % (tq % 2)

        def inproj_fm(l, name):
            c_off, width = cfg.off[name]
            post = FM_POST[name]
            rk = ROPE_SLOTS.get(name)
            half = 64 if rk == 0 else 32
            gw = 256 if post == "rope" else 512
            dst = sc[name]
            for g0 in range(0, width, gw):
                wg = min(gw, width - g0)
                weff = wg
                load_w_cast(Wt[:, :, 0:wg], w_in[l, :, c_off + g0:c_off + g0 + wg], "Wt")
                if name == "a_ik":
                    load_w_cast(Wt[:, :, 64:128], w_in[l, :, c_off:c_off + 64], "Wt")
                    weff = 128
                if post == "rope":
                    for s0 in range(0, weff, 2 * half):
                        cs = c_off + g0 + (s0 % wg)
                        load_w_cast(Wt[:, :, 256 + s0:256 + s0 + half], w_in[l, :, cs + half:cs + 2 * half], "Wt")
                        load_w_cast(Wt[:, :, 256 + s0 + half:256 + s0 + 2 * half], w_in[l, :, cs:cs + half], "Wt")
                for tq in range(NQC):
                    if tq == 0:
                        load_ht(0)
                    if tq + 1 < NQC:
                        load_ht(tq + 1)
                    ht, htn = HT[tq % 2], "HT%d" % (tq % 2)
                    if post == "rope":
                        pg.dma("sp", rope[:, 0:2, :], rope_t[rk:rk + 2, :, tq * 512:(tq + 1) * 512].rearrange("m p t -> p m t"),
                               writes=["rope"])
                    for ti, t0 in enumerate(range(0, weff, 128)):
                        tw = min(128, weff - t0)
                        pa, pan = pA[ti % 2], "pA%d" % (ti % 2)
                        for kc in range(KC):
                            pg.op("pe", lambda e, kc=kc, pa=pa, t0=t0, tw=tw, ht=ht: e.matmul(
                                pa[0:tw, :], lhsT=Wt[:, kc, t0:t0 + tw], rhs=ht[:, kc, :], start=(kc == 0), stop=(kc == KC - 1)),
                                reads=["Wt", htn], writes=[pan])
                        rows = slice(g0 + t0, g0 + t0 + tw)
                        cols = slice(tq * 512, (tq + 1) * 512)
                        if post == "rope":
                            pa2, pan2 = pA[2 + ti % 2], "pA%d" % (2 + ti % 2)
                            for kc in range(KC):
                                pg.op("pe", lambda e, kc=kc, pa2=pa2, t0=t0, tw=tw, ht=ht: e.matmul(
                                    pa2[0:tw, :], lhsT=Wt[:, kc, 256 + t0:256 + t0 + tw], rhs=ht[:, kc, :],
                                    start=(kc == 0), stop=(kc == KC - 1)), reads=["Wt", htn], writes=[pan2])
                            f1, f1n = nxt("wf")
                            f2, f2n = nxt("wf")
                            ob, obn = nxt("wb")
                            pg.op("dve", lambda e, f1=f1, pa=pa, tw=tw: e.tensor_tensor(out=f1[0:tw, 0:512], in0=pa[0:tw, :],
                                                                                       in1=rope[0:tw, 0, :], op=ALU.mult),
                                  reads=[pan, "rope"], writes=[f1n])
                            pg.op("dve", lambda e, f2=f2, pa2=pa2, tw=tw: e.tensor_tensor(out=f2[0:tw, 0:512], in0=pa2[0:tw, :],
                                                                                         in1=rope[0:tw, 1, :], op=ALU.mult),
                                  reads=[pan2, "rope"], writes=[f2n])
                            pg.op("pool", lambda e, f1=f1, f2=f2, ob=ob, tw=tw: e.tensor_tensor(out=ob[0:tw, 0:512], in0=f1[0:tw, 0:512],
                                                                                                in1=f2[0:tw, 0:512], op=ALU.add),
                                  reads=[f1n, f2n], writes=[obn])
                            pg.dma("sp", dst[rows, cols], ob[0:tw, 0:512], reads=[obn], writes=[dst.name])
                        elif post == "silu":
                            ob, obn = nxt("wb")
                            pg.op("act", lambda e, ob=ob, pa=pa, tw=tw: e.activation(out=ob[0:tw, 0:512], in_=pa[0:tw, :], func=AF.Silu),
                                  reads=[pan], writes=[obn])
                            pg.dma("sp", dst[rows, cols], ob[0:tw, 0:512], reads=[obn], writes=[dst.name])
                        elif post == "bf16":
                            ob, obn = nxt("wb")
                            pg.op("dve", lambda e, ob=ob, pa=pa, tw=tw: e.tensor_copy(out=ob[0:tw, 0:512], in_=pa[0:tw, :]),
                                  reads=[pan], writes=[obn])
                            pg.dma("sp", dst[rows, cols], ob[0:tw, 0:512], reads=[obn], writes=[dst.name])
                        else:
                            of, ofn = nxt("wf")
                            pg.op("act", lambda e, of=of, pa=pa, tw=tw: e.copy(out=of[0:tw, 0:512], in_=pa[0:tw, :]),
                                  reads=[pan], writes=[ofn])
                            pg.dma("sp", dst[rows, cols], of[0:tw, 0:512], reads=[ofn], writes=[dst.name])

        def inproj_tm(l, name):
            c_off, width = cfg.off[name]
            dst = sc[name]
            for g0 in range(0, width, 512):
                wg = min(512, width - g0)
                load_w_cast(Wt[:, :, 0:wg], w_in[l, :, c_off + g0:c_off + g0 + wg], "Wt")
                for tq in range(NQC):
                    if tq == 0:
                        load_ht(0)
                    if tq + 1 < NQC:
                        load_ht(tq + 1)
                    ht, htn = HT[tq % 2], "HT%d" % (tq % 2)
                    for j in range(4):
                        pa, pan = pA[j % 2], "pA%d" % (j % 2)
                        for kc in range(KC):
                            pg.op("pe", lambda e, kc=kc, pa=pa, j=j, ht=ht, wg=wg: e.matmul(
                                pa[:, 0:wg], lhsT=ht[:, kc, j * P:(j + 1) * P], rhs=Wt[:, kc, 0:wg],
                                start=(kc == 0), stop=(kc == KC - 1)), reads=["Wt", htn], writes=[pan])
                        rows = slice((tq * 4 + j) * P, (tq * 4 + j + 1) * P)
                        if name == "a_iw":
                            of, ofn = nxt("wf")
                            pg.op("dve", lambda e, of=of, pa=pa, wg=wg: e.tensor_scalar(
                                out=of[:, 0:wg], in0=pa[:, 0:wg], scalar1=float(cfg.IH ** -0.5 * cfg.ID ** -0.5), scalar2=None, op0=ALU.mult),
                                reads=[pan], writes=[ofn])
                            pg.dma("sp", dst[rows, g0:g0 + wg], of[:, 0:wg], reads=[ofn], writes=[dst.name])
                        else:
                            ob, obn = nxt("wb")
                            pg.op("act" if j % 2 else "dve", (lambda e, ob=ob, pa=pa, wg=wg: e.copy(out=ob[:, 0:wg], in_=pa[:, 0:wg])) if j % 2 else
                                  (lambda e, ob=ob, pa=pa, wg=wg: e.tensor_copy(out=ob[:, 0:wg], in_=pa[:, 0:wg])),
                                  reads=[pan], writes=[obn])
                            pg.dma("sp", dst[rows, g0:g0 + wg], ob[:, 0:wg], reads=[obn], writes=[dst.name])

        def stage_inproj(l):
            for name in FM_POST:
                inproj_fm(l, name)
            for name in ("a_v", "a_iw", "c_v", "d_i"):
                inproj_tm(l, name)

        def stage_conv(l):
            load_cols(colp[:, 0:3 * H], conv_w[l, :, :].rearrange("k (h p) -> (k h) p", p=P), 3 * H)
            for h in range(H):
                rows = slice(h * P, (h + 1) * P)
                for tq in range(NQC):
                    t0 = tq * 512
                    bc, bcn = nxt("wf")
                    bx, bxn = nxt("wf")
                    bb, bbn = nxt("wf")
                    yv, yvn = nxt("wf")
                    bg, bgn = nxt("wb")
                    ob, obn = nxt("wb")
                    if tq == 0:
                        pg.op("dve", lambda e, bc=bc: e.memset(bc[:, 0:2], 0.0), writes=[bcn])
                        pg.op("dve", lambda e, bx=bx: e.memset(bx[:, 0:2], 0.0), writes=[bxn])
                        pg.dma("sp", bc[:, 2:514], sc["b_c"][rows, 0:512], reads=["s_b_c"], writes=[bcn])
                        pg.dma("sp", bx[:, 2:514], sc["b_x"][rows, 0:512], reads=["s_b_x"], writes=[bxn])
                    else:
                        pg.dma("sp", bc[:, 0:514], sc["b_c"][rows, t0 - 2:t0 + 512], reads=["s_b_c"], writes=[bcn])
                        pg.dma("sp", bx[:, 0:514], sc["b_x"][rows, t0 - 2:t0 + 512], reads=["s_b_x"], writes=[bxn])
                    pg.dma("sp", bb[:, 0:512], sc["b_b"][rows, t0:t0 + 512], reads=["s_b_b"], writes=[bbn])
                    pg.dma("sp", bg[:, 0:512], sc["b_g"][rows, t0:t0 + 512], reads=["s_b_g"], writes=[bgn])
                    pg.op("dve", lambda e, bc=bc, bx=bx: e.tensor_tensor(out=bc[:, 0:514], in0=bc[:, 0:514], in1=bx[:, 0:514], op=ALU.mult),
                          reads=[bcn, bxn], writes=[bcn])
                    pg.op("dve", lambda e, bc=bc, yv=yv, h=h: e.tensor_scalar(out=yv[:, 0:512], in0=bc[:, 2:514],
                                                                              scalar1=colp[:, 2 * H + h:2 * H + h + 1], scalar2=None, op0=ALU.mult),
                          reads=[bcn, "colp"], writes=[yvn])
                    pg.op("dve", lambda e, bc=bc, yv=yv, h=h: e.scalar_tensor_tensor(out=yv[:, 0:512], in0=bc[:, 1:513],
                                                                                     scalar=colp[:, H + h:H + h + 1], in1=yv[:, 0:512],
                                                                                     op0=ALU.mult, op1=ALU.add),
                          reads=[bcn, "colp", yvn], writes=[yvn])
                    pg.op("dve", lambda e, bc=bc, yv=yv, h=h: e.scalar_tensor_tensor(out=yv[:, 0:512], in0=bc[:, 0:512],
                                                                                     scalar=colp[:, h:h + 1], in1=yv[:, 0:512],
                                                                                     op0=ALU.mult, op1=ALU.add),
                          reads=[bcn, "colp", yvn], writes=[yvn])
                    pg.op("pool", lambda e, yv=yv, bb=bb: e.tensor_tensor(out=yv[:, 0:512], in0=yv[:, 0:512], in1=bb[:, 0:512], op=ALU.mult),
                          reads=[yvn, bbn], writes=[yvn])
                    pg.op("pool", lambda e, yv=yv, bg=bg, ob=ob: e.tensor_tensor(out=ob[:, 0:512], in0=yv[:, 0:512], in1=bg[:, 0:512], op=ALU.mult),
                          reads=[yvn, bgn], writes=[obn])
                    pg.dma("sp", yT[BW + h * P:BW + (h + 1) * P, t0:t0 + 512], ob[:, 0:512], reads=[obn], writes=["yT"])

        def stage_fox(l):
            pg.dma("sp", SC[0:H, 0:S], sc["c_f"][:, :], reads=["s_c_f"], writes=["SC"])
            pg.dma("sp", small[0:H, 8:9], fox_b[l:l + 1, :].rearrange("o h -> h o"), writes=["small"])
            pg.op("act", lambda e: e.activation(out=SC[0:H, 0:S], in_=SC[0:H, 0:S], func=AF.Sigmoid, bias=small[0:H, 8:9], scale=1.0),
                  reads=["SC", "small"], writes=["SC"])
            pg.op("act", lambda e: e.activation(out=SC[0:H, 0:S], in_=SC[0:H, 0:S], func=AF.Ln), reads=["SC"], writes=["SC"])
            for tq in range(NQC):
                ini = 0.0 if tq == 0 else SC[0:H, tq * 512 - 1:tq * 512]
                pg.op("dve", lambda e, tq=tq, ini=ini: e.tensor_tensor_scan(out=SC[0:H, tq * 512:(tq + 1) * 512], data0=onesf[0:H, 0:512],
                                                                            data1=SC[0:H, tq * 512:(tq + 1) * 512], initial=ini,
                                                                            op0=ALU.mult, op1=ALU.add),
                      reads=["SC", "onesf"], writes=["SC"])
            pg.dma("sp", ccum[:, :], SC[0:H, 0:S], reads=["SC"], writes=["ccum"])
            for tt in range(NTT):
                pg.op("pe", lambda e, tt=tt: e.transpose(out=pB[0][:, 0:H], in_=SC[0:H, tt * P:(tt + 1) * P], identity=idf[0:H, 0:H]),
                      reads=["SC", "idf"], writes=["pB0"])
                pg.op("dve", lambda e, tt=tt: e.tensor_scalar(out=negc[:, tt, :], in0=pB[0][:, 0:H], scalar1=-1.0, scalar2=None, op0=ALU.mult),
                      reads=["pB0"], writes=["negc"])
            qf, kf = HTf[0][:, 0:S], HTf[0][:, S:2 * S]
            vf, gf = HTf[1][:, 0:S], HTf[1][:, S:2 * S]
            v3 = vf.rearrange("p (a b) -> p a b", b=P)
            scale = 128.0 ** -0.5
            for h in range(H):
                rows = slice(h * P, (h + 1) * P)
                pg.dma("sp", qf, sc["c_q"][rows, :], reads=["s_c_q"], writes=["HT0"])
                pg.dma("sp", kf, sc["c_k"][rows, :], reads=["s_c_k"], writes=["HT0"])
                pg.dma("sp", v3, sc["c_v"][:, rows].rearrange("(a p) e -> p a e", p=P), reads=["s_c_v"], writes=["HT1"])
                pg.dma("sp", gf, sc["c_g"][rows, :], reads=["s_c_g"], writes=["HT1"])
                for qc in range(NQC):
                    cq, cqn = cqt, "cqt"
                    pg.dma("sp", cq[:, 0:512], ccum[h:h + 1, qc * 512:(qc + 1) * 512].to_broadcast([P, 512]), reads=["ccum"], writes=[cqn])
                    nkt = 4 * (qc + 1)
                    for kt in range(nkt):
                        pa, pan = pA[kt % 2], "pA%d" % (kt % 2)
                        pg.op("pe", lambda e, pa=pa, kt=kt, qc=qc: e.matmul(pa[:, :], lhsT=kf[:, kt * P:(kt + 1) * P],
                                                                          rhs=qf[:, qc * 512:(qc + 1) * 512], start=True, stop=True),
                              reads=["HT0"], writes=[pan])
                        z, zn = nxt("wf")
                        pg.op("dve", lambda e, z=z, pa=pa, cq=cq: e.scalar_tensor_tensor(out=z[:, 0:512], in0=pa[:, :], scalar=scale,
                                                                                       in1=cq[:, 0:512], op0=ALU.mult, op1=ALU.add),
                              reads=[pan, cqn], writes=[zn])
                        if kt >= 4 * qc:
                            j = kt - 4 * qc
                            pg.op("pool", lambda e, z=z, j=j: e.tensor_tensor(out=z[:, 0:512], in0=z[:, 0:512], in1=cm_fm[:, j, :], op=ALU.add),
                                  reads=[zn, "cm_fm"], writes=[zn])
                        pt, ptn = nxt("wb")
                        pg.op("act", lambda e, pt=pt, z=z, kt=kt, h=h: e.activation(out=pt[:, 0:512], in_=z[:, 0:512], func=AF.Exp,
                                                                                  bias=negc[:, kt, h:h + 1], scale=1.0),
                              reads=[zn, "negc"], writes=[ptn])
                        pg.op("pe", lambda e, pt=pt, kt=kt, nkt=nkt: e.matmul(pB[0][:, :], lhsT=v3[:, kt, :], rhs=pt[:, 0:512],
                                                                            start=(kt == 0), stop=(kt == nkt - 1)),
                              reads=[ptn, "HT1"], writes=["pB0"])
                        pg.op("pe", lambda e, pt=pt, kt=kt, nkt=nkt: e.matmul(pB[1][:, :], lhsT=onesb[:, :], rhs=pt[:, 0:512],
                                                                            start=(kt == 0), stop=(kt == nkt - 1)),
                              reads=[ptn, "onesb"], writes=["pB1"])
                    rs, rsn = nxt("wf")
                    ov, ovn = nxt("wf")
                    ob, obn = nxt("wb")
                    pg.op("dve", lambda e, rs=rs: e.reciprocal(out=rs[:, 0:512], in_=pB[1][:, :]), reads=["pB1"], writes=[rsn])
                    pg.op("dve", lambda e, rs=rs, ov=ov: e.tensor_tensor(out=ov[:, 0:512], in0=pB[0][:, :], in1=rs[:, 0:512], op=ALU.mult),
                          reads=["pB0", rsn], writes=[ovn])
                    pg.op("pool", lambda e, ov=ov, ob=ob, qc=qc: e.tensor_tensor(out=ob[:, 0:512], in0=ov[:, 0:512],
                                                                                in1=gf[:, qc * 512:(qc + 1) * 512], op=ALU.mult),
                          reads=[ovn, "HT1"], writes=[obn])
                    pg.dma("sp", yT[2 * BW + h * P:2 * BW + (h + 1) * P, qc * 512:(qc + 1) * 512], ob[:, 0:512], reads=[obn], writes=["yT"])

        def stage_hgrn(l):
            load_cols(colp[:, 0:H], hnorm_w[l, :].rearrange("(h p) -> h p", p=P), H)
            NC8 = 8
            for h in range(H):
                rows = slice(h * P, (h + 1) * P)
                lbc = lbcol[:, l * H + h:l * H + h + 1]
                omc = lbcol[:, L * H + l * H + h:L * H + l * H + h + 1]
                pg.op("dve", lambda e: e.memset(Sst[:, :], 0.0), writes=["Sst"])
                pg.op("dve", lambda e: e.memset(Sbf[:, :], 0.0), writes=["Sbf"])
                for tq in range(NQC):
                    cols = slice(tq * 512, (tq + 1) * 512)
                    f_, fn_ = nxt("wf")
                    q_, qn_ = nxt("wf")
                    g_, gn_ = nxt("wf")
                    k_, kn_ = nxt("wf")
                    b_, bn_ = nxt("wf")
                    e_, en_ = nxt("wf")
                    qt, qtn = nxt("wb")
                    kt, ktn = nxt("wb")
                    kh, khn = nxt("wb")
                    vt, vtn = nxt("wb")
                    pg.dma("sp", f_[:, 0:512], sc["d_f"][rows, cols], reads=["s_d_f"], writes=[fn_])
                    pg.dma("sp", q_[:, 0:512], sc["d_q"][rows, cols], reads=["s_d_q"], writes=[qn_])
                    vt3 = vt[0:64, 0:NC8 * P].rearrange("p (a b) -> p a b", b=P)
                    pg.dma("sp", vt3, sc["d_i"][tq * 512:(tq + 1) * 512, rows].rearrange("(c s) e -> s c e", s=64),
                           reads=["s_d_i"], writes=[vtn])
                    pg.op("act", lambda e, f_=f_: e.activation(out=f_[:, 0:512], in_=f_[:, 0:512], func=AF.Sigmoid), reads=[fn_], writes=[fn_])
                    pg.op("dve", lambda e, f_=f_, omc=omc, lbc=lbc: e.tensor_scalar(out=f_[:, 0:512], in0=f_[:, 0:512], scalar1=omc, scalar2=lbc,
                                                                  op0=ALU.mult, op1=ALU.add), reads=[fn_, "lbcol"], writes=[fn_])
                    pg.op("act", lambda e, f_=f_, g_=g_: e.activation(out=g_[:, 0:512], in_=f_[:, 0:512], func=AF.Ln), reads=[fn_], writes=[gn_])
                    pg.op("dve", lambda e, f_=f_, k_=k_: e.tensor_scalar(out=k_[:, 0:512], in0=f_[:, 0:512], scalar1=-1.0, scalar2=1.0,
                                                                         op0=ALU.mult, op1=ALU.add), reads=[fn_], writes=[kn_])
                    for c in range(NC8):
                        pg.op("dve", lambda e, c=c, b_=b_, g_=g_: e.tensor_tensor_scan(out=b_[:, c * 64:(c + 1) * 64], data0=onesf[:, 0:64],
                                                                                       data1=g_[:, c * 64:(c + 1) * 64], initial=0.0,
                                                                                       op0=ALU.mult, op1=ALU.add),
                              reads=[gn_, "onesf"], writes=[bn_])
                    pg.op("act", lambda e, e_=e_, b_=b_: e.activation(out=e_[:, 0:512], in_=b_[:, 0:512], func=AF.Exp), reads=[bn_], writes=[en_])
                    pg.op("dve", lambda e, e_=e_, q_=q_, qt=qt: e.tensor_tensor(out=qt[:, 0:512], in0=q_[:, 0:512], in1=e_[:, 0:512], op=ALU.mult),
                          reads=[en_, qn_], writes=[qtn])
                    pg.op("act", lambda e, e_=e_, b_=b_: e.activation(out=e_[:, 0:512], in_=b_[:, 0:512], func=AF.Exp, scale=-1.0),
                          reads=[bn_, qtn], writes=[en_])
                    pg.op("dve", lambda e, e_=e_, k_=k_, kt=kt: e.tensor_tensor(out=kt[:, 0:512], in0=k_[:, 0:512], in1=e_[:, 0:512], op=ALU.mult),
                          reads=[en_, kn_], writes=[ktn])
                    for c in range(NC8):
                        pg.op("act", lambda e, c=c, b_=b_, g_=g_: e.activation(out=g_[:, c * 64:(c + 1) * 64], in_=b_[:, c * 64:(c + 1) * 64],
                                                                               func=AF.Exp, bias=b_[:, c * 64 + 63:c * 64 + 64], scale=-1.0),
                              reads=[bn_, gn_], writes=[gn_])
                    pg.op("dve", lambda e, g_=g_, k_=k_, kh=kh: e.tensor_tensor(out=kh[:, 0:512], in0=k_[:, 0:512], in1=g_[:, 0:512], op=ALU.mult),
                          reads=[gn_, kn_], writes=[khn])
                    pg.op("act", lambda e, b_=b_: e.activation(out=small[:, 16:16 + NC8], in_=b_[:, 63:512:64], func=AF.Exp),
                          reads=[bn_], writes=["small"])
                    for c in range(NC8):
                        pg.op("pe", lambda e, c=c, kh=kh: e.transpose(out=pT[0][0:64, c * P:(c + 1) * P], in_=kh[:, c * 64:(c + 1) * 64],
                                                                     identity=idb[:, :]), reads=[khn, "idb"], writes=["pT0"])
                    khT, khTn = nxt("wb")
                    pg.op("dve", lambda e, khT=khT: e.tensor_copy(out=khT[0:64, 0:NC8 * P], in_=pT[0][0:64, 0:NC8 * P]), reads=["pT0"], writes=[khTn])
                    for c in range(NC8):
                        cs = slice(c * 64, (c + 1) * 64)
                        pg.op("pe", lambda e, cs=cs, kt=kt, qt=qt: e.matmul(pA[0][0:64, 0:64], lhsT=kt[:, cs], rhs=qt[:, cs], start=True, stop=True),
                              reads=[ktn, qtn], writes=["pA0"])
                        pg.op("dve", lambda e: e.tensor_tensor(out=att_bf[:, :], in0=pA[0][0:64, 0:64], in1=tril[:, :], op=ALU.mult),
                              reads=["pA0", "tril"], writes=["att_bf"])
                        pg.op("pe", lambda e, cs=cs, qt=qt: e.matmul(pB[0][:, cs], lhsT=Sbf[:, :], rhs=qt[:, cs], start=True, stop=False),
                              reads=["Sbf", qtn], writes=["pB0"])
                        pg.op("pe", lambda e, cs=cs, c=c, vt3=vt3: e.matmul(pB[0][:, cs], lhsT=vt3[:, c, :], rhs=att_bf[:, :], start=False, stop=True),
                              reads=[vtn, "att_bf"], writes=["pB0"])
                        pg.op("pe", lambda e, c=c, khT=khT, vt3=vt3: e.matmul(pA[1][:, 0:P], lhsT=khT[0:64, c * P:(c + 1) * P], rhs=vt3[:, c, :],
                                                                            start=True, stop=True), reads=[khTn, vtn], writes=["pA1"])
                        pg.op("dve", lambda e, c=c: e.scalar_tensor_tensor(out=Sst[:, :], in0=Sst[:, :], scalar=small[:, 16 + c:17 + c],
                                                                           in1=pA[1][:, 0:P], op0=ALU.mult, op1=ALU.add),
                              reads=["Sst", "small", "pA1"], writes=["Sst"])
                        pg.op("act", lambda e: e.copy(out=Sbf[:, :], in_=Sst[:, :]), reads=["Sst"], writes=["Sbf"])
                    ov, ovn = nxt("wf")
                    sq, sqn = nxt("wb")
                    gt, gtn = nxt("wb")
                    ob, obn = nxt("wb")
                    rs, rsn = nxt("wf")
                    pg.dma("sp", gt[:, 0:512], sc["d_g"][rows, cols], reads=["s_d_g"], writes=[gtn])
                    pg.op("act", lambda e, ov=ov: e.copy(out=ov[:, 0:512], in_=pB[0][:, :]), reads=["pB0"], writes=[ovn])
                    if getattr(cfg, "hdbg", 0):
                        src_t = {1: ov, 2: b_, 3: k_, 4: q_, 5: qt, 6: kt, 7: kh}[cfg.hdbg]
                        pg.op("dve", lambda e, ov=src_t, ob=ob: e.tensor_copy(out=ob[:, 0:512], in_=ov[:, 0:512]), reads=[ovn, bn_, kn_, qn_, qtn, ktn, khn], writes=[obn])
                        pg.dma("sp", yT[3 * BW + h * P:3 * BW + (h + 1) * P, cols], ob[:, 0:512], reads=[obn], writes=["yT"])
                        continue
                    pg.op("dve", lambda e, ov=ov, sq=sq: e.tensor_tensor(out=sq[:, 0:512], in0=ov[:, 0:512], in1=ov[:, 0:512], op=ALU.mult),
                          reads=[ovn], writes=[sqn])
                    pg.op("pe", lambda e, sq=sq: e.matmul(pB[1][:, :], lhsT=onesb[:, :], rhs=sq[:, 0:512], start=True, stop=True),
                          reads=[sqn, "onesb"], writes=["pB1"])
                    pg.op("dve", lambda e, rs=rs: e.tensor_scalar(out=rs[:, 0:512], in0=pB[1][:, :], scalar1=1.0 / 128.0, scalar2=1e-6,
                                                                  op0=ALU.mult, op1=ALU.add), reads=["pB1"], writes=[rsn])
                    pg.op("act", lambda e, rs=rs: e.activation(out=rs[:, 0:512], in_=rs[:, 0:512], func=AF.Sqrt), reads=[rsn], writes=[rsn])
                    pg.op("dve", lambda e, rs=rs: e.reciprocal(out=rs[:, 0:512], in_=rs[:, 0:512]), reads=[rsn], writes=[rsn])
                    pg.op("dve", lambda e, rs=rs, ov=ov: e.tensor_tensor(out=ov[:, 0:512], in0=ov[:, 0:512], in1=rs[:, 0:512], op=ALU.mult),
                          reads=[rsn, ovn], writes=[ovn])
                    pg.op("dve", lambda e, ov=ov, gt=gt, ob=ob, h=h: e.scalar_tensor_tensor(out=ob[:, 0:512], in0=ov[:, 0:512], scalar=colp[:, h:h + 1],
                                                                                          in1=gt[:, 0:512], op0=ALU.mult, op1=ALU.mult),
                          reads=[ovn, gtn, "colp"], writes=[obn])
                    pg.dma("sp", yT[3 * BW + h * P:3 * BW + (h + 1) * P, cols], ob[:, 0:512], reads=[obn], writes=["yT"])

        def stage_dsa(l):
            NIT = 18
            K = float(cfg.TOPK)
            ikf, kaf = HTf[0][:, 0:S], HTf[0][:, S:2 * S]
            va3 = HTf[1][:, 0:S].rearrange("p (a b) -> p a b", b=P)
            mq = HTf[1][:, S:2 * S]
            pg.dma("sp", ikf, sc["a_ik"][:, :], reads=["s_a_ik"], writes=["HT0"])
            pg.dma("sp", kaf, sc["a_k"][:, :], reads=["s_a_k"], writes=["HT0"])
            pg.dma("sp", va3, sc["a_v"][:, :].rearrange("(a p) e -> p a e", p=P), reads=["s_a_v"], writes=["HT1"])
            iq_v = sc["a_iq"].ap().rearrange("(a p) t -> p a t", p=P)
            qa_v = sc["a_q"].ap().rearrange("(a p) t -> p a t", p=P)
            ga_v = sc["a_g"].ap().rearrange("(a p) t -> p a t", p=P)
            ya_v = yT[0:BW, :].rearrange("(a p) t -> p a t", p=P)
            HW = H * P
            nhalf = (HW + 511) // 512
            hw2 = HW // nhalf
            scale = 128.0 ** -0.5
            for qb in range(NTT):
                qs = slice(qb * P, (qb + 1) * P)
                nk = (qb + 1) * P
                iqb = dsa_iq[:, :].rearrange("p (a b) -> p a b", b=P)
                qab = dsa_qa[:, 0:HW].rearrange("p (a b) -> p a b", b=P)
                gab = dsa_ga[:, 0:HW].rearrange("p (a b) -> p a b", b=P)
                pg.dma("sp", iqb, iq_v[:, :, qs], reads=["s_a_iq"], writes=["Wt"])
                pg.dma("sp", qab, qa_v[:, :, qs], reads=["s_a_q"], writes=["Wt"])
                pg.dma("sp", gab, ga_v[:, :, qs], reads=["s_a_g"], writes=["Wt"])
                pg.dma("sp", iwt[:, :], sc["a_iw"][qs, :], reads=["s_a_iw"], writes=["iwt"])
                for k0 in range(0, nk, 512):
                    kw = min(512, nk - k0)
                    for hh in range(16):
                        pr, hf = hh // 2, hh % 2
                        pa, pan = pA[hh % 4], "pA%d" % (hh % 4)
                        pg.op("pe", lambda e, pa=pa, pr=pr, hf=hf, k0=k0, kw=kw, iqb=iqb: e.matmul(
                            pa[:, 0:kw], lhsT=iqb[hf * 64:(hf + 1) * 64, pr, :], rhs=ikf[hf * 64:(hf + 1) * 64, k0:k0 + kw], start=True, stop=True),
                            reads=["Wt", "HT0"], writes=[pan])
                        r_, rn_ = nxt("wf")
                        pg.op("act", lambda e, r_=r_, pa=pa, kw=kw: e.activation(out=r_[:, 0:kw], in_=pa[:, 0:kw], func=AF.Relu),
                              reads=[pan], writes=[rn_])
                        if hh == 0:
                            pg.op("dve", lambda e, r_=r_, k0=k0, kw=kw: e.tensor_scalar(out=SC[:, k0:k0 + kw], in0=r_[:, 0:kw], scalar1=iwt[:, 0:1],
                                                                                        scalar2=None, op0=ALU.mult),
                                  reads=[rn_, "iwt"], writes=["SC"])
                        else:
                            pg.op("dve", lambda e, r_=r_, k0=k0, kw=kw, hh=hh: e.scalar_tensor_tensor(out=SC[:, k0:k0 + kw], in0=r_[:, 0:kw],
                                                                                                     scalar=iwt[:, hh:hh + 1], in1=SC[:, k0:k0 + kw],
                                                                                                     op0=ALU.mult, op1=ALU.add),
                                  reads=[rn_, "iwt", "SC"], writes=["SC"])
                if qb >= 2:
                    pg.op("dve", lambda e, nk=nk: e.tensor_reduce(out=small[:, 32:33], in_=SC[:, 0:nk], axis=AX.X, op=ALU.max,
                                                                  apply_absolute_value=True), reads=["SC"], writes=["small"])
                pg.op("dve", lambda e, qs=qs: e.tensor_tensor(out=SC[:, qs], in0=SC[:, qs], in1=cm_tm[:, :], op=ALU.add),
                      reads=["SC", "cm_tm"], writes=["SC"])
                if qb >= 2:
                    pg.op("dve", lambda e: e.tensor_scalar(out=small[:, 33:34], in0=small[:, 32:33], scalar1=-1.001, scalar2=-1e-6,
                                                           op0=ALU.mult, op1=ALU.add), reads=["small"], writes=["small"])
                    pg.op("dve", lambda e: e.tensor_scalar(out=small[:, 34:35], in0=small[:, 33:34], scalar1=-2.0, scalar2=None, op0=ALU.mult),
                          reads=["small"], writes=["small"])
                    for it in range(NIT):
                        fac = 2.0 ** -(it + 1)
                        pg.op("dve", lambda e, fac=fac: e.tensor_scalar(out=small[:, 35:36], in0=small[:, 34:35], scalar1=fac, scalar2=None, op0=ALU.mult),
                              reads=["small"], writes=["small"])
                        pg.op("dve", lambda e: e.tensor_tensor(out=small[:, 36:37], in0=small[:, 33:34], in1=small[:, 35:36], op=ALU.add),
                              reads=["small"], writes=["small"])
                        pg.op("dve", lambda e, nk=nk: e.tensor_scalar(out=mq[:, 0:nk], in0=SC[:, 0:nk], scalar1=small[:, 36:37], scalar2=0.0,
                                                                      op0=ALU.is_gt, op1=ALU.add, accum_out=small[:, 37:38]),
                              reads=["SC", "small"], writes=["HT1", "small"])
                        pg.op("dve", lambda e: e.tensor_scalar(out=small[:, 38:39], in0=small[:, 37:38], scalar1=K - 0.5, scalar2=None, op0=ALU.is_gt),
                              reads=["small"], writes=["small"])
                        pg.op("dve", lambda e: e.scalar_tensor_tensor(out=small[:, 33:34], in0=small[:, 38:39], scalar=small[:, 35:36],
                                                                      in1=small[:, 33:34], op0=ALU.mult, op1=ALU.add),
                              reads=["small"], writes=["small"])
                else:
                    pg.op("dve", lambda e: e.memset(small[:, 33:34], -1e29), reads=["small"], writes=["small"])
                pg.op("dve", lambda e, nk=nk: e.tensor_scalar(out=mq[:, 0:nk], in0=SC[:, 0:nk], scalar1=small[:, 33:34], scalar2=None, op0=ALU.is_gt),
                      reads=["SC", "small"], writes=["HT1"])
                for k0 in range(0, qb + 1, 8):
                    nkk = min(8, qb + 1 - k0)
                    pt, ptn = pT[(k0 // 8) % 2], "pT%d" % ((k0 // 8) % 2)
                    for kk in range(nkk):
                        pg.op("pe", lambda e, pt=pt, kk=kk, k0=k0: e.transpose(out=pt[:, kk * P:(kk + 1) * P],
                                                                             in_=mq[:, (k0 + kk) * P:(k0 + kk + 1) * P], identity=idb[:, :]),
                              reads=["HT1", "idb"], writes=[ptn])
                    pg.op("act", lambda e, pt=pt, k0=k0, nkk=nkk: e.copy(out=maskT[:, k0:k0 + nkk, :].rearrange("p a b -> p (a b)"),
                                                                        in_=pt[:, 0:nkk * P]), reads=[ptn], writes=["Wt"])
                qaf = dsa_qa[:, 0:HW]
                for kt in range(qb + 1):
                    last = (kt == qb)
                    for hf in range(nhalf):
                        cs = slice(hf * hw2, (hf + 1) * hw2)
                        pg.op("pe", lambda e, hf=hf, kt=kt, cs=cs: e.matmul(pA[hf][:, 0:hw2], lhsT=kaf[:, kt * P:(kt + 1) * P], rhs=qaf[:, cs],
                                                                          start=True, stop=True), reads=["HT0", "Wt"], writes=["pA%d" % hf])
                    pt_, ptn_ = nxt("wb")
                    for hf in range(nhalf):
                        cs = slice(hf * hw2, (hf + 1) * hw2)
                        pg.op("act", lambda e, hf=hf, cs=cs, pt_=pt_: e.activation(out=pt_[:, cs], in_=pA[hf][:, 0:hw2], func=AF.Exp, scale=scale),
                              reads=["pA%d" % hf], writes=[ptn_])
                    p3 = pt_[:, 0:HW].rearrange("p (a b) -> p a b", b=P)
                    pg.op("dve", lambda e, p3=p3, kt=kt: e.tensor_tensor(out=p3, in0=p3, in1=maskT[:, kt:kt + 1, :].to_broadcast([P, H, P]), op=ALU.mult),
                          reads=[ptn_, "Wt"], writes=[ptn_])
                    for hf in range(nhalf):
                        cs = slice(hf * hw2, (hf + 1) * hw2)
                        pg.op("pe", lambda e, hf=hf, cs=cs, kt=kt, pt_=pt_, last=last: e.matmul(pB[hf][:, 0:hw2], lhsT=va3[:, kt, :], rhs=pt_[:, cs],
                                                                                             start=(kt == 0), stop=last),
                              reads=[ptn_, "HT1"], writes=["pB%d" % hf])
                        pg.op("pe", lambda e, hf=hf, cs=cs, kt=kt, pt_=pt_, last=last: e.matmul(pA[2 + hf][:, 0:hw2], lhsT=onesb[:, :], rhs=pt_[:, cs],
                                                                                             start=(kt == 0), stop=last),
                              reads=[ptn_, "onesb"], writes=["pA%d" % (2 + hf)])
                ob, obn = nxt("wb")
                for hf in range(nhalf):
                    cs = slice(hf * hw2, (hf + 1) * hw2)
                    rs, rsn = nxt("wf")
                    ov, ovn = nxt("wf")
                    pg.op("dve", lambda e, rs=rs, hf=hf: e.reciprocal(out=rs[:, 0:hw2], in_=pA[2 + hf][:, 0:hw2]), reads=["pA%d" % (2 + hf)], writes=[rsn])
                    pg.op("dve", lambda e, rs=rs, ov=ov, hf=hf: e.tensor_tensor(out=ov[:, 0:hw2], in0=pB[hf][:, 0:hw2], in1=rs[:, 0:hw2], op=ALU.mult),
                          reads=["pB%d" % hf, rsn], writes=[ovn])
                    pg.op("pool", lambda e, ov=ov, ob=ob, cs=cs: e.tensor_tensor(out=ob[:, cs], in0=ov[:, 0:hw2], in1=dsa_ga[:, cs], op=ALU.mult),
                          reads=[ovn, "Wt"], writes=[obn])
                pg.dma("sp", ya_v[:, :, qs], ob[:, 0:HW].rearrange("p (a b) -> p a b", b=P), reads=[obn], writes=["yT"])

        def stage_merge(l):
            for i in range(4):
                load_cols(colp[:, i * KC:(i + 1) * KC], b_merge[l, i:i + 1, :].rearrange("o (kc p) -> (o kc) p", p=P), KC)
            nkk = bd // P
            CG = 2
            nbr = 4 * H * P
            nmg = 4 * nkk * P
            ychs = [h_[:, 0:4 * H * 512].rearrange("p (a b) -> p a b", b=512) for h_ in HTf]
            yT_v = yT.ap().rearrange("(a p) t -> p a t", p=P)
            for cg in range(0, KC, CG):
                wbrs, wmgs = [], []
                for ci in range(CG):
                    ct = cg + ci
                    n = (ct * P) // bd
                    c0 = ct * P - n * bd
                    base = ci * (nbr + nmg)
                    wbr = Wtf[:, base:base + nbr].rearrange("p (a b) -> p a b", b=P)
                    wmg = Wtf[:, base + nbr:base + nbr + nmg].rearrange("p (a b) -> p a b", b=P)
                    for i in range(4):
                        load_w_cast(wbr[:, i * H:(i + 1) * H, :], w_branch[l, i, :, ct * P:(ct + 1) * P], "Wt")
                        load_w_cast(wmg[:, i * nkk:(i + 1) * nkk, :], w_merge[l, i, n, :, c0:c0 + P], "Wt")
                    wbrs.append(wbr)
                    wmgs.append(wmg)

                def ld(tq):
                    pg.dma("sp", ychs[tq % 2], yT_v[:, :, tq * 512:(tq + 1) * 512], reads=["yT"], writes=["HT%d" % (tq % 2)])
                for tq in range(NQC):
                    cols = slice(tq * 512, (tq + 1) * 512)
                    if tq == 0:
                        ld(0)
                    if tq + 1 < NQC:
                        ld(tq + 1)
                    ych, ychn = ychs[tq % 2], "HT%d" % (tq % 2)
                    for ci in range(CG):
                        ct = cg + ci
                        n = (ct * P) // bd
                        wbr, wmg = wbrs[ci], wmgs[ci]
                        hc, hcn = nxt("wb")
                        hch = hc[:, 0:nkk * 512].rearrange("p (a b) -> p a b", b=512)
                        pg.dma("sp", hch, hT_v[:, (n * bd) // P:(n * bd) // P + nkk, cols], reads=["hT"], writes=[hcn])
                        acc, accn = nxt("wf")
                        for i in range(4):
                            pa, pan = pA[i % 2], "pA%d" % (i % 2)
                            pb, pbn = pB[i % 2], "pB%d" % (i % 2)
                            for kk in range(nkk):
                                pg.op("pe", lambda e, pa=pa, i=i, kk=kk, wmg=wmg, hch=hch: e.matmul(pa[:, :], lhsT=wmg[:, i * nkk + kk, :], rhs=hch[:, kk, :],
                                                                                                  start=(kk == 0), stop=(kk == nkk - 1)),
                                      reads=["Wt", hcn], writes=[pan])
                            for f in range(H):
                                pg.op("pe", lambda e, pb=pb, i=i, f=f, wbr=wbr, ych=ych: e.matmul(pb[:, :], lhsT=wbr[:, i * H + f, :], rhs=ych[:, i * H + f, :],
                                                                                                start=(f == 0), stop=(f == H - 1)),
                                      reads=["Wt", ychn], writes=[pbn])
                            gt, gtn = nxt("wf")
                            pg.op("act", lambda e, gt=gt, pa=pa, i=i, ct=ct: e.activation(out=gt[:, 0:512], in_=pa[:, :], func=AF.Sigmoid,
                                                                                        bias=colp[:, i * KC + ct:i * KC + ct + 1], scale=1.0),
                                  reads=[pan, "colp"], writes=[gtn])
                            if i == 0:
                                pg.op("dve", lambda e, gt=gt, pb=pb, acc=acc: e.tensor_tensor(out=acc[:, 0:512], in0=gt[:, 0:512], in1=pb[:, :], op=ALU.mult),
                                      reads=[gtn, pbn], writes=[accn])
                            else:
                                pg.op("dve", lambda e, gt=gt, pb=pb: e.tensor_tensor(out=gt[:, 0:512], in0=gt[:, 0:512], in1=pb[:, :], op=ALU.mult),
                                      reads=[gtn, pbn], writes=[gtn])
                                pg.op("pool", lambda e, gt=gt, acc=acc: e.tensor_tensor(out=acc[:, 0:512], in0=acc[:, 0:512], in1=gt[:, 0:512], op=ALU.add),
                                      reads=[gtn, accn], writes=[accn])
                        ob, obn = nxt("wb")
                        pg.op("act", lambda e, ob=ob, acc=acc: e.copy(out=ob[:, 0:512], in_=acc[:, 0:512]), reads=[accn], writes=[obn])
                        pg.dma("sp", mT[ct * P:(ct + 1) * P, cols], ob[:, 0:512], reads=[obn], writes=["mT"])

        def stage_out(l):
            for g0 in range(0, D, 512):
                load_w_cast(Wt[:, :, :], w_out[l, :, g0:g0 + 512], "Wt")
                for tq in range(NQC):
                    if tq == 0:
                        load_ht(0, mT_v, "mT")
                    if tq + 1 < NQC:
                        load_ht(tq + 1, mT_v, "mT")
                    mt, mtn = HT[tq % 2], "HT%d" % (tq % 2)
                    for j in range(4):
                        pa, pan = pA[j % 2], "pA%d" % (j % 2)
                        rows = slice((tq * 4 + j) * P, (tq * 4 + j + 1) * P)
                        xv, xvn = nxt("wf")
                        pg.dma("sp", xv[:, 0:512], xres[rows, g0:g0 + 512], reads=["xres"], writes=[xvn])
                        for kc in range(KC):
                            pg.op("pe", lambda e, kc=kc, pa=pa, j=j, mt=mt: e.matmul(pa[:, :], lhsT=mt[:, kc, j * P:(j + 1) * P], rhs=Wt[:, kc, :],
                                                                                   start=(kc == 0), stop=(kc == KC - 1)),
                                  reads=["Wt", mtn], writes=[pan])
                        pg.op("dve", lambda e, xv=xv, pa=pa: e.tensor_tensor(out=xv[:, 0:512], in0=xv[:, 0:512], in1=pa[:, :], op=ALU.add),
                              reads=[xvn, pan], writes=[xvn])
                        pg.dma("sp", xres[rows, g0:g0 + 512], xv[:, 0:512], reads=[xvn], writes=["xres"])

        STAGES = cfg.stages if hasattr(cfg, "stages") else ("inproj", "conv", "fox", "hgrn", "dsa", "merge", "out")
        for l in range(L):
            stage_norm(xres, norm_w[l:l + 1, :])
            if "inproj" in STAGES:
                stage_inproj(l)
            if "conv" in STAGES:
                stage_conv(l)
            if "fox" in STAGES:
                stage_fox(l)
            if "hgrn" in STAGES:
                stage_hgrn(l)
            if "dsa" in STAGES:
                stage_dsa(l)
            if "merge" in STAGES:
                stage_merge(l)
            if "out" in STAGES:
                stage_out(l)
        stage_norm(xres, fin_w[0:1, :], final=True)
        dbg = []
        for n in getattr(cfg, "dump", ()):
            src = {"hT": hT, "yT": yT, "mT": mT, "xres": xres, "ccum": ccum}.get(n) or sc[n]
            o = nc.dram_tensor("dbg_" + n, list(src.shape), src.dtype, kind="ExternalOutput")
            rows = src.shape[0]
            for r0 in range(0, rows, 1024):
                r1 = min(rows, r0 + 1024)
                pg.dma("sp", o[r0:r1, :], src[r0:r1, :], reads=[src.name], writes=["y"])
        pg.finish_wait("sp", ["y"])
        pg.emit()
    return nc


_CACHE = {}


def const_inputs(cfg):
    S = cfg.S
    import ml_dtypes
    pos = np.arange(S, dtype=np.float32)
    tabs = np.zeros((4, P, S), np.float32)
    for ti, half in ((0, 64), (2, 32)):
        inv = (10000.0 ** (-np.arange(half, dtype=np.float32) / half)).astype(np.float32)
        ang = pos[None, :] * inv[:, None]
        cos, sin = np.cos(ang), np.sin(ang)
        reps = P // (2 * half)
        tabs[ti] = np.concatenate([cos, cos] * reps, axis=0)
        tabs[ti + 1] = np.concatenate([-sin, sin] * reps, axis=0)
    k = np.arange(P)[:, None]
    cm_fm = np.zeros((4, P, 512), np.float32)
    for j in range(4):
        q = np.arange(512)[None, :] - 128 * j
        cm_fm[j] = np.where(q >= k, 0.0, NEG)
    cm_tm = np.where(np.arange(P)[None, :] > np.arange(P)[:, None], -1e30, 0.0).astype(np.float32)
    tril = (np.arange(64)[None, :] >= np.arange(64)[:, None]).astype(np.float32)
    return dict(rope_t=tabs, ident_b=np.eye(P, dtype=np.float32).astype(ml_dtypes.bfloat16),
                ident_f=np.eye(P, dtype=np.float32), cmask_fm=cm_fm.astype(ml_dtypes.bfloat16), cmask_tm=cm_tm, tril01=tril)


def run(cfg, inputs):
    key = (cfg.S, cfg.D, cfg.DEPTH)
    if key not in _CACHE:
        _CACHE[key] = build_nc(cfg)
    nc = _CACHE[key]
    B = inputs["x"].shape[0]
    consts = const_inputs(cfg)
    in_maps = []
    for b in range(B):
        m = dict(consts)
        m["x"] = np.ascontiguousarray(inputs["x"][b])
        for n in ("norm_w", "w_in", "fox_f_bias", "conv_w", "hgrn_gamma", "hgrn_norm_w", "w_branch", "w_merge",
                  "b_merge", "w_out"):
            m[n] = np.ascontiguousarray(inputs[n])
        m["final_norm_w"] = np.ascontiguousarray(inputs["final_norm_w"]).reshape(1, -1)
        in_maps.append(m)
    res = run_bass_kernel_spmd(nc, in_maps, core_ids=list(range(B)))
    if getattr(cfg, "dump", None):
        cfg.dbg = [{k: v for k, v in res.results[b].items() if k.startswith("dbg_")} for b in range(B)]
    return np.stack([res.results[b]["y"] for b in range(B)], axis=0)


def run_unfused(inputs, S=8192, D=4096, LT=4):
    cfg = Cfg(S=S, D=D, DEPTH=1)
    cfg.LT, cfg.unfused, cfg.dump = LT, True, ("xres",)
    key = ("unfused", cfg.S, cfg.D)
    if key not in _CACHE:
        _CACHE[key] = build_nc(cfg)
    nc = _CACHE[key]
    B = inputs["x"].shape[0]
    consts = const_inputs(cfg)
    xs = [np.ascontiguousarray(inputs["x"][b]) for b in range(B)]
    ys = None
    for l in range(LT):
        sel = np.zeros((P, LT), np.float32)
        sel[:, 1:l + 1] = 1.0
        in_maps = []
        for b in range(B):
            m = dict(consts)
            m["x"] = xs[b]
            for n in ("norm_w", "w_in", "fox_f_bias", "conv_w", "hgrn_norm_w", "w_branch", "w_merge", "b_merge", "w_out"):
                m[n] = np.ascontiguousarray(inputs[n][l:l + 1])
            m["hgrn_gamma"] = np.ascontiguousarray(inputs["hgrn_gamma"])
            m["selm"] = sel
            m["final_norm_w"] = np.ascontiguousarray(inputs["final_norm_w"]).reshape(1, -1)
            in_maps.append(m)
        res = run_bass_kernel_spmd(nc, in_maps, core_ids=list(range(B)))
        xs = [np.ascontiguousarray(res.results[b]["dbg_xres"]) for b in range(B)]
        ys = [res.results[b]["y"] for b in range(B)]
    return np.stack(ys, axis=0)


FUSED = False


def kernel(**inputs):
    inputs = {k: np.asarray(v) for k, v in inputs.items()}
    if not FUSED:
        return run_unfused(inputs)
    return run(Cfg(), inputs)
```
